# Optimizing a Trainium2 kernel written in Bass

```python
import jax
import jax.numpy as jnp
from jax import lax
import numpy as np

D_MODEL = 4096
BATCH = 2
SEQ = 8192
DEPTH = 1

GRID_W = 64
CTX_LEN = 256

D_RWKV = D_MODEL // 2
RWKV_HEAD = 64
RWKV_HEADS = D_RWKV // RWKV_HEAD
DECAY_LORA = 96
ICLR_LORA = 96
GATE_LORA = 256
GN_EPS = 64e-5

ATT_HEAD = 128
ATT_HEADS = (D_MODEL // 2) // ATT_HEAD
ATT_KV_HEADS = 4
ATT_GROUPS = ATT_HEADS // ATT_KV_HEADS
D_ATT = ATT_HEADS * ATT_HEAD
D_ATT_KV = ATT_KV_HEADS * ATT_HEAD
WINDOW = 128
BLOCK = 128
ROPE_BASE = 10000.0

D_FF = -(-(8 * D_MODEL) // (3 * 256)) * 256

LN_EPS = 1e-5
DEEPNORM_ALPHA = (2 * DEPTH) ** 0.25
DEEPNORM_BETA = (8 * DEPTH) ** -0.25
NEG_INF = -1e30

N_RWKV_COLS = 3 * D_RWKV + 2 * DECAY_LORA + 2 * ICLR_LORA + GATE_LORA
N_IN = N_RWKV_COLS + D_ATT + 2 * D_ATT_KV + 2 * D_MODEL
IN_SPLIT = (N_RWKV_COLS, N_RWKV_COLS + D_ATT, N_RWKV_COLS + D_ATT + D_ATT_KV,
            N_RWKV_COLS + D_ATT + 2 * D_ATT_KV, N_RWKV_COLS + D_ATT + 2 * D_ATT_KV + D_MODEL)
RWKV_SPLIT = (D_RWKV, 2 * D_RWKV, 3 * D_RWKV, 3 * D_RWKV + 2 * DECAY_LORA,
              3 * D_RWKV + 2 * DECAY_LORA + 2 * ICLR_LORA)

kernel_name = 'hybrid_rwkv7_swa_dit_block'


def layer_norm(x, g, b):
    xf = x.astype(jnp.float32)
    mu = jnp.mean(xf, -1, keepdims=True)
    var = jnp.mean(jnp.square(xf - mu), -1, keepdims=True)
    return ((xf - mu) * lax.rsqrt(var + LN_EPS) * g + b).astype(x.dtype)


def centred_conv3(x, w):
    xp = jnp.pad(x, ((0, 0), (1, 1), (0, 0)))
    return xp[:, :-2] * w[0] + xp[:, 1:-1] * w[1] + xp[:, 2:] * w[2]


def axial_rope(T):
    rows = T // GRID_W
    row = jnp.broadcast_to(jnp.arange(rows, dtype=jnp.float32)[:, None], (rows, GRID_W)).reshape(T)
    col = jnp.broadcast_to(jnp.arange(GRID_W, dtype=jnp.float32)[None, :], (rows, GRID_W)).reshape(T)
    axis_dim = ATT_HEAD // 2
    inv = ROPE_BASE ** (-jnp.arange(0, axis_dim, 2, dtype=jnp.float32) / axis_dim)
    ang = jnp.concatenate([row[:, None] * inv, col[:, None] * inv], -1)
    return jnp.cos(ang)[None, :, None, :], jnp.sin(ang)[None, :, None, :]


def apply_rope(x, cos, sin):
    xf = x.astype(jnp.float32)
    x1, x2 = xf[..., :ATT_HEAD // 2], xf[..., ATT_HEAD // 2:]
    return jnp.concatenate([x1 * cos - x2 * sin, x2 * cos + x1 * sin], -1).astype(x.dtype)


def heads64(t):
    return t.reshape(t.shape[:-1] + (RWKV_HEADS, RWKV_HEAD))


def rwkv_features(cols, shift_w, w0, w_up, a0, a_up, g_up, k_k, k_a):
    B, T = cols.shape[:2]
    x = centred_conv3(cols, shift_w)
    r, k, v, wd, ad, gd = jnp.split(x, RWKV_SPLIT, axis=-1)
    wd = wd.reshape(B, T, 2, DECAY_LORA)
    ad = ad.reshape(B, T, 2, ICLR_LORA)
    w_log = -jax.nn.softplus(-(w0 + jnp.einsum('btdr,drc->btdc', jnp.tanh(wd), w_up))) - 0.5
    decay = jnp.exp(-jnp.exp(w_log.astype(jnp.float32)))
    iclr = jax.nn.sigmoid(a0 + jnp.einsum('btdr,drc->btdc', ad, a_up))
    kk = heads64((k * k_k).astype(jnp.float32))
    kk = kk / jnp.maximum(jnp.linalg.norm(kk, axis=-1, keepdims=True), 1e-12)
    k_dir = k[:, :, None, :] * (1.0 + (iclr - 1.0) * k_a)
    b = kk[:, :, None] * heads64(iclr).astype(jnp.float32)
    g = jax.nn.sigmoid(gd) @ g_up
    return heads64(r), heads64(v), g, heads64(decay), heads64(k_dir), kk, b


def rwkv7_scan(state, r, decay, k, v, kk, b, reverse, with_outputs):
    def step(S, inp):
        r_t, w_t, k_t, v_t, kk_t, b_t = inp
        sa = jnp.einsum('bhvk,bhk->bhv', S, kk_t)
        S = S * w_t[:, :, None, :] - sa[..., None] * b_t[:, :, None, :] + v_t[..., None] * k_t[:, :, None, :]
        if with_outputs:
            return S, jnp.einsum('bhvk,bhk->bhv', S, r_t)
        return S, None
    xs = tuple(jnp.moveaxis(t.astype(jnp.float32), 1, 0) for t in (r, decay, k, v, kk, b))
    S, ys = lax.scan(step, state, xs, reverse=reverse)
    if with_outputs:
        ys = jnp.moveaxis(ys, 0, 1)
    return S, ys


def rwkv_output(y_fwd, y_bwd, r, k_dir, v, g, r_k, gn_g, gn_b):
    B, T = g.shape[:2]
    y = y_fwd + y_bwd
    mu = jnp.mean(y, -1, keepdims=True)
    var = jnp.mean(jnp.square(y - mu), -1, keepdims=True)
    yn = ((y - mu) * lax.rsqrt(var + GN_EPS)).reshape(B, T, D_RWKV) * gn_g + gn_b
    k_sum = (k_dir[:, :, 0] + k_dir[:, :, 1]).astype(jnp.float32)
    bonus = jnp.sum(r.astype(jnp.float32) * k_sum * r_k, -1, keepdims=True) * v.astype(jnp.float32)
    return ((yn + bonus.reshape(B, T, D_RWKV)) * g).astype(g.dtype)


def windowed_attention(q, k, v, k_ctx, v_ctx, sink):
    B, T = q.shape[:2]
    L = k_ctx.shape[1]
    nb = T // BLOCK
    qb = q.reshape(B, nb, BLOCK, ATT_KV_HEADS, ATT_GROUPS, ATT_HEAD)
    pad = ((0, 0), (BLOCK, BLOCK), (0, 0), (0, 0))
    kp = jnp.pad(k, pad).reshape(B, nb + 2, BLOCK, ATT_KV_HEADS, ATT_HEAD)
    vp = jnp.pad(v, pad).reshape(B, nb + 2, BLOCK, ATT_KV_HEADS, ATT_HEAD)
    band = lambda t: jnp.concatenate([t[:, :-2], t[:, 1:-1], t[:, 2:]], axis=2)
    kw, vw = band(kp), band(vp)
    scale = ATT_HEAD ** -0.5
    s_win = jnp.einsum('bnqhgd,bnkhd->bnhgqk', qb, kw).astype(jnp.float32) * scale
    q_off = jnp.arange(BLOCK)[:, None]
    k_off = jnp.arange(3 * BLOCK)[None, :] - BLOCK
    in_window = jnp.abs(q_off - k_off) <= WINDOW
    k_abs = jnp.arange(nb)[:, None] * BLOCK + k_off
    in_seq = (k_abs >= 0) & (k_abs < T)
    mask = in_window[None] & in_seq[:, None, :]
    s_win = jnp.where(mask[None, :, None, None], s_win, NEG_INF)
    s_ctx = jnp.einsum('bnqhgd,blhd->bnhgql', qb, k_ctx).astype(jnp.float32) * scale
    s_sink = jnp.broadcast_to(sink.astype(jnp.float32).reshape(ATT_KV_HEADS, ATT_GROUPS)[None, None, :, :, None, None],
                              s_win.shape[:-1] + (1,))
    p = jax.nn.softmax(jnp.concatenate([s_win, s_ctx, s_sink], -1), axis=-1).astype(v.dtype)
    nw = 3 * BLOCK
    o = (jnp.einsum('bnhgqk,bnkhd->bnqhgd', p[..., :nw], vw)
         + jnp.einsum('bnhgql,blhd->bnqhgd', p[..., nw:nw + L], v_ctx))
    return o.reshape(B, T, D_ATT)


def context_attention(q_c, k_c, v_c, sink):
    B, L = q_c.shape[:2]
    qg = q_c.reshape(B, L, ATT_KV_HEADS, ATT_GROUPS, ATT_HEAD)
    s = jnp.einsum('bqhgd,bkhd->bhgqk', qg, k_c).astype(jnp.float32) * ATT_HEAD ** -0.5
    s_sink = jnp.broadcast_to(sink.astype(jnp.float32).reshape(ATT_KV_HEADS, ATT_GROUPS)[None, :, :, None, None],
                              s.shape[:-1] + (1,))
    p = jax.nn.softmax(jnp.concatenate([s, s_sink], -1), axis=-1)[..., :L].astype(v_c.dtype)
    return jnp.einsum('bhgqk,bkhd->bqhgd', p, v_c).reshape(B, L, D_ATT)


def merge_branches(o_rwkv, o_att, gate_r, gate_a, w_rwkv_o, w_att_o, w_out):
    y = jax.nn.sigmoid(gate_r) * (o_rwkv @ w_rwkv_o) + jax.nn.sigmoid(gate_a) * (o_att @ w_att_o)
    return y @ w_out


def token_mixer(h, h_c, cos, sin, w_in, shift_w, w0, w_up, a0, a_up, g_up, k_k, k_a, r_k, gn_g, gn_b,
                sink, w_rwkv_o, w_att_o, w_out, ctx_out):
    B, T = h.shape[:2]
    L = h_c.shape[1]
    rw, q, k, v, gate_r, gate_a = jnp.split(h @ w_in, IN_SPLIT, axis=-1)
    rw_c, q_c, k_c, v_c, gate_r_c, gate_a_c = jnp.split(h_c @ w_in, IN_SPLIT, axis=-1)

    r, vr, g, decay, k_dir, kk, b = rwkv_features(rw, shift_w, w0, w_up, a0, a_up, g_up, k_k, k_a)
    r_c, vr_c, g_c, decay_c, k_dir_c, kk_c, b_c = rwkv_features(rw_c, shift_w, w0, w_up, a0, a_up, g_up, k_k, k_a)
    zero = jnp.zeros((B, RWKV_HEADS, RWKV_HEAD, RWKV_HEAD), jnp.float32)
    ys, ys_c = [], []
    for d in range(2):
        rev = d == 1
        S_c, y_c = rwkv7_scan(zero, r_c, decay_c[:, :, d], k_dir_c[:, :, d], vr_c, kk_c, b_c[:, :, d], rev, ctx_out)
        _, y = rwkv7_scan(S_c, r, decay[:, :, d], k_dir[:, :, d], vr, kk, b[:, :, d], rev, True)
        ys.append(y)
        ys_c.append(y_c)
    o_rwkv = rwkv_output(ys[0], ys[1], r, k_dir, vr, g, r_k, gn_g, gn_b)

    qh = apply_rope(q.reshape(B, T, ATT_HEADS, ATT_HEAD), cos, sin)
    kh = apply_rope(k.reshape(B, T, ATT_KV_HEADS, ATT_HEAD), cos, sin)
    vh = v.reshape(B, T, ATT_KV_HEADS, ATT_HEAD)
    kc = k_c.reshape(B, L, ATT_KV_HEADS, ATT_HEAD)
    vc = v_c.reshape(B, L, ATT_KV_HEADS, ATT_HEAD)
    o_att = windowed_attention(qh, kh, vh, kc, vc, sink)

    y = merge_branches(o_rwkv, o_att, gate_r, gate_a, w_rwkv_o, w_att_o, w_out)
    if not ctx_out:
        return y, None
    o_rwkv_c = rwkv_output(ys_c[0], ys_c[1], r_c, k_dir_c, vr_c, g_c, r_k, gn_g, gn_b)
    o_att_c = context_attention(q_c.reshape(B, L, ATT_HEADS, ATT_HEAD), kc, vc, sink)
    y_c = merge_branches(o_rwkv_c, o_att_c, gate_r_c, gate_a_c, w_rwkv_o, w_att_o, w_out)
    return y, y_c


def swiglu(h, w_gate, w_up, w_down):
    return (jax.nn.silu(h @ w_gate) * (h @ w_up)) @ w_down


def setup_inputs(seed: int = 0) -> dict:
    key = jax.random.key(seed)
    ks = iter(jax.random.split(key, 40))

    def nrm(shape, scale):
        return scale * jax.random.normal(next(ks), shape, jnp.float32)

    C = D_RWKV
    return {
        'x': nrm((BATCH, SEQ, D_MODEL), 1.0),
        'c': nrm((BATCH, D_MODEL), 1.0),
        'ctx': nrm((BATCH, CTX_LEN, D_MODEL), 1.0),
        'c_ctx': nrm((D_MODEL,), 1.0),
        'w_ada': nrm((DEPTH, D_MODEL, 6 * D_MODEL), 0.5 * D_MODEL ** -0.5),
        'b_ada': nrm((DEPTH, 6 * D_MODEL), 0.02),
        'w_in': nrm((DEPTH, D_MODEL, N_IN), D_MODEL ** -0.5),
        'rwkv_shift': nrm((DEPTH, 3, N_RWKV_COLS), 0.2) + jnp.array([0.0, 1.0, 0.0], jnp.float32)[None, :, None],
        'rwkv_w0': jax.random.uniform(next(ks), (DEPTH, 2, C), jnp.float32, -6.0, -1.0),
        'rwkv_w_up': nrm((DEPTH, 2, DECAY_LORA, C), 0.3 * DECAY_LORA ** -0.5),
        'rwkv_a0': nrm((DEPTH, 2, C), 0.5),
        'rwkv_a_up': nrm((DEPTH, 2, ICLR_LORA, C), ICLR_LORA ** -0.5),
        'rwkv_g_up': nrm((DEPTH, GATE_LORA, C), GATE_LORA ** -0.5),
        'rwkv_k_k': 0.85 + nrm((DEPTH, C), 0.05),
        'rwkv_k_a': 1.0 + nrm((DEPTH, C), 0.05),
        'rwkv_r_k': nrm((DEPTH, RWKV_HEADS, RWKV_HEAD), 0.1),
        'rwkv_gn_g': 1.0 + nrm((DEPTH, C), 0.02),
        'rwkv_gn_b': nrm((DEPTH, C), 0.02),
        'attn_sink': nrm((DEPTH, ATT_HEADS), 0.5),
        'w_rwkv_o': nrm((DEPTH, C, D_MODEL), C ** -0.5),
        'w_att_o': nrm((DEPTH, D_ATT, D_MODEL), D_ATT ** -0.5),
        'w_out': nrm((DEPTH, D_MODEL, D_MODEL), DEEPNORM_BETA * D_MODEL ** -0.5),
        'ln1_g': 1.0 + nrm((DEPTH, D_MODEL), 0.02),
        'ln1_b': nrm((DEPTH, D_MODEL), 0.02),
        'w_ff_gate': nrm((DEPTH, D_MODEL, D_FF), D_MODEL ** -0.5),
        'w_ff_up': nrm((DEPTH, D_MODEL, D_FF), D_MODEL ** -0.5),
        'w_ff_down': nrm((DEPTH, D_FF, D_MODEL), DEEPNORM_BETA * D_FF ** -0.5),
        'ln2_g': 1.0 + nrm((DEPTH, D_MODEL), 0.02),
        'ln2_b': nrm((DEPTH, D_MODEL), 0.02),
    }


def reference(x, c, ctx, c_ctx, w_ada, b_ada, w_in, rwkv_shift, rwkv_w0, rwkv_w_up, rwkv_a0, rwkv_a_up,
              rwkv_g_up, rwkv_k_k, rwkv_k_a, rwkv_r_k, rwkv_gn_g, rwkv_gn_b, attn_sink, w_rwkv_o, w_att_o,
              w_out, ln1_g, ln1_b, w_ff_gate, w_ff_up, w_ff_down, ln2_g, ln2_b):
    T = x.shape[1]
    cos, sin = axial_rope(T)
    x_c = ctx
    for l in range(DEPTH):
        last = l == DEPTH - 1
        sh1, sc1, gt1, sh2, sc2, gt2 = jnp.split((jax.nn.silu(c) @ w_ada[l] + b_ada[l])[:, None, :], 6, axis=-1)
        sh1c, sc1c, gt1c, sh2c, sc2c, gt2c = jnp.split(jax.nn.silu(c_ctx) @ w_ada[l] + b_ada[l], 6, axis=-1)

        h = x * (1.0 + sc1) + sh1
        h_c = x_c * (1.0 + sc1c) + sh1c
        y, y_c = token_mixer(h, h_c, cos, sin, w_in[l], rwkv_shift[l], rwkv_w0[l], rwkv_w_up[l], rwkv_a0[l],
                             rwkv_a_up[l], rwkv_g_up[l], rwkv_k_k[l], rwkv_k_a[l], rwkv_r_k[l], rwkv_gn_g[l],
                             rwkv_gn_b[l], attn_sink[l], w_rwkv_o[l], w_att_o[l], w_out[l], not last)
        x = layer_norm(DEEPNORM_ALPHA * x + gt1 * y, ln1_g[l], ln1_b[l])

        h = x * (1.0 + sc2) + sh2
        x = layer_norm(DEEPNORM_ALPHA * x + gt2 * swiglu(h, w_ff_gate[l], w_ff_up[l], w_ff_down[l]), ln2_g[l], ln2_b[l])

        if not last:
            x_c = layer_norm(DEEPNORM_ALPHA * x_c + gt1c * y_c, ln1_g[l], ln1_b[l])
            h_c = x_c * (1.0 + sc2c) + sh2c
            x_c = layer_norm(DEEPNORM_ALPHA * x_c + gt2c * swiglu(h_c, w_ff_gate[l], w_ff_up[l], w_ff_down[l]),
                             ln2_g[l], ln2_b[l])
    return x
```

```python
import contextlib
import numpy as np
import concourse.bass as bass
import concourse.mybir as mybir
from concourse.bass_utils import run_bass_kernel_spmd

F32 = mybir.dt.float32
BF16 = mybir.dt.bfloat16
AF = mybir.ActivationFunctionType
ALU = mybir.AluOpType
AX = mybir.AxisListType

SAME_SYNC = True
NDMA = 40


class Tl:
    def __init__(self, h, name=""):
        self.h = h
        self.name = name
        self.last_w = None
        self.readers = {}

    def __getitem__(self, idx):
        return self.h[idx]


class _RecInst:
    def __init__(self, item):
        self.item = item

    def then_inc(self, sem, n):
        self.item[3] = (sem, n)
        return self


class _Rec:
    def __init__(self, name):
        self._name = name
        self._items = []

    def __getattr__(self, meth):
        if meth.startswith("_"):
            raise AttributeError(meth)

        def f(*a, **k):
            item = [meth, a, k, None]
            self._items.append(item)
            return _RecInst(item)
        return f


class Prog:
    def __init__(self, nc):
        self.nc = nc
        self.real = {"pe": nc.tensor, "act": nc.scalar, "dve": nc.vector, "pool": nc.gpsimd, "sp": nc.sync}
        self.eng = {k: _Rec(k) for k in self.real}
        self.sem = {k: nc.alloc_semaphore("s_" + k) for k in ["pe", "act", "dve", "pool"]}
        self.cnt = {k: 0 for k in self.sem}
        self.waited = {k: {} for k in self.eng}
        self.dma_sems = [nc.alloc_semaphore("s_dma%d" % i) for i in range(NDMA)]
        self.dma_cnt = [0] * NDMA
        self.dma_rr = 0
        self.semobj = {}
        for k, s in self.sem.items():
            self.semobj[id(s)] = s
        for s in self.dma_sems:
            self.semobj[id(s)] = s
        self.n_inst = 0
        self.stack = None

    def sb(self, name, shape, dt):
        return Tl(self.stack.enter_context(self.nc.sbuf_tensor(name, list(shape), dt)), name)

    def ps(self, name, shape, dt=F32):
        return Tl(self.stack.enter_context(self.nc.psum_tensor(name, list(shape), dt)), name)

    def view(self, h, name=""):
        return Tl(h, name)

    @contextlib.contextmanager
    def phase(self):
        old = self.stack
        with contextlib.ExitStack() as st:
            self.stack = st
            yield
            self.barrier()
            self.flush()
        self.stack = old

    def barrier(self):
        allv = {}
        for k, s in self.sem.items():
            if self.cnt[k] > 0:
                allv[id(s)] = self.cnt[k]
        for j, s in enumerate(self.dma_sems):
            if self.dma_cnt[j] > 0:
                allv[id(s)] = self.dma_cnt[j]
        for e in self.eng:
            for k, v in allv.items():
                if self.waited[e].get(k, 0) < v:
                    self.eng[e].wait_ge(self.semobj[k], v)
                    self.waited[e][k] = v

    def _deps(self, reads, writes):
        deps = {}

        def add(tok):
            if tok is not None:
                k = id(tok[0])
                if deps.get(k, 0) < tok[1]:
                    deps[k] = tok[1]
        for t in reads:
            add(t.last_w)
        for t in writes:
            add(t.last_w)
            for k, v in t.readers.items():
                if deps.get(k, 0) < v:
                    deps[k] = v
        return deps

    def _waits(self, e, deps, same_sync):
        own = id(self.sem[e]) if e in self.sem else None
        for k, v in deps.items():
            if self.waited[e].get(k, 0) >= v:
                continue
            if k == own and not same_sync:
                continue
            self.eng[e].wait_ge(self.semobj[k], v)
            self.waited[e][k] = v

    def _commit(self, tok, reads, writes):
        k = id(tok[0])
        for t in reads:
            if t.readers.get(k, 0) < tok[1]:
                t.readers[k] = tok[1]
        for t in writes:
            t.last_w = tok
            t.readers = {}

    def op(self, e, fn, reads=(), writes=(), same_sync=None):
        if same_sync is None:
            same_sync = SAME_SYNC
        deps = self._deps(reads, writes)
        self._waits(e, deps, same_sync)
        self.cnt[e] += 1
        tok = (self.sem[e], self.cnt[e])
        fn(self.eng[e]).then_inc(tok[0], 1)
        self._commit(tok, reads, writes)
        self.n_inst += 1
        return tok

    def group(self, e, fns, reads=(), writes=()):
        deps = self._deps(reads, writes)
        self._waits(e, deps, SAME_SYNC)
        n = len(fns)
        for i, fn in enumerate(fns):
            inst = fn(self.eng[e])
            if i == n - 1:
                self.cnt[e] += 1
                tok = (self.sem[e], self.cnt[e])
                inst.then_inc(tok[0], 1)
        self._commit(tok, reads, writes)
        self.n_inst += n
        return tok

    def dma(self, q, fn, reads=(), writes=()):
        j = self.dma_rr
        self.dma_rr = (j + 1) % NDMA
        sem = self.dma_sems[j]
        deps = self._deps(reads, writes)
        if self.dma_cnt[j] > 0:
            k = id(sem)
            if deps.get(k, 0) < self.dma_cnt[j]:
                deps[k] = self.dma_cnt[j]
        self._waits(q, deps, True)
        self.dma_cnt[j] += 16
        tok = (sem, self.dma_cnt[j])
        fn(self.eng[q]).then_inc(sem, 16)
        self._commit(tok, reads, writes)
        self.n_inst += 1
        return tok

    def flush(self):
        recs = self.eng
        if not any(r._items for r in recs.values()):
            return
        real = self.real

        def replay(k):
            e = real[k]
            for meth, a, kw, inc in recs[k]._items:
                inst = getattr(e, meth)(*a, **kw)
                if inc is not None:
                    inst.then_inc(inc[0], inc[1])
            recs[k]._items = []

        with self.nc.Block() as block:
            @block.sync
            def _(e):
                replay("sp")

            @block.tensor
            def _(e):
                replay("pe")

            @block.scalar
            def _(e):
                replay("act")

            @block.vector
            def _(e):
                replay("dve")

            @block.gpsimd
            def _(e):
                replay("pool")

    def run(self, body):
        with self.phase():
            body(self)
        return self.nc

    def act(self, out, in_, func, reads, writes, bias=None, scale=None):
        kw = {}
        if bias is not None:
            kw["bias"] = bias
        if scale is not None:
            kw["scale"] = scale
        return self.op("act", lambda e: e.activation(out=out, in_=in_, func=func, **kw), reads, writes)

    def tt(self, eng, out, in0, in1, op, reads, writes):
        return self.op(eng, lambda e: e.tensor_tensor(out=out, in0=in0, in1=in1, op=op), reads, writes)

    def ts(self, eng, out, in0, s1, op0, reads, writes, s2=None, op1=None):
        if op1 is None:
            return self.op(eng, lambda e: e.tensor_scalar(out=out, in0=in0, scalar1=s1, scalar2=None, op0=op0), reads, writes)
        return self.op(eng, lambda e: e.tensor_scalar(out=out, in0=in0, scalar1=s1, scalar2=s2, op0=op0, op1=op1), reads, writes)

    def stt(self, out, in0, scalar, in1, op0, op1, reads, writes):
        return self.op("dve", lambda e: e.scalar_tensor_tensor(out=out, in0=in0, scalar=scalar, in1=in1, op0=op0, op1=op1), reads, writes)

    def copy(self, eng, out, in_, reads, writes):
        if eng == "act":
            return self.op("act", lambda e: e.copy(out=out, in_=in_), reads, writes)
        return self.op(eng, lambda e: e.tensor_copy(out=out, in_=in_), reads, writes)

    def mm(self, ps_tl, mms, reads):
        n = len(mms)
        fns = [(lambda e, i=i, m=m: e.matmul(m[0], m[1], m[2], start=(i == 0), stop=(i == n - 1))) for i, m in enumerate(mms)]
        return self.group("pe", fns, reads=reads, writes=[ps_tl])


def _mmx(self, ps_tl, mms, reads):
    fns = [(lambda e, m=m: e.matmul(m[0], m[1], m[2], start=m[3], stop=m[4])) for m in mms]
    return self.group("pe", fns, reads=reads, writes=[ps_tl])


Prog.mmx = _mmx


import numpy as np

D = 4096
DT = 32
DFF = 11008
FT = 86
ALPHA = float(2 ** 0.25)
LN_EPS = 1e-5


def build_L0():
    nc = bass.Bass("TRN2", target_bir_lowering=False)
    cT = nc.dram_tensor("cT", [128, 32, 4], F32, kind="ExternalInput").ap()
    w = nc.dram_tensor("w", [4096, 3072], F32, kind="ExternalInput").ap()
    b = nc.dram_tensor("b", [128, 24], F32, kind="ExternalInput").ap()
    out = nc.dram_tensor("mod", [128, 24, 4], F32, kind="ExternalOutput").ap()
    P = Prog(nc)
    P.run(lambda P: _L0_body(P, cT, w, b, out))
    return nc


def _L0_body(P, cT, w, b, out):
    wv = w.rearrange("(kc p) c -> p kc c", p=128)
    ct = P.sb("ct", [128, 32, 4], F32)
    st = P.sb("st", [128, 32, 4], F32)
    bt = P.sb("bt", [128, 24], F32)
    ot = P.sb("ot", [128, 24, 4], F32)
    wb = [P.sb("wb%d" % i, [128, 32, 128], F32) for i in range(3)]
    ps = [P.ps("ps%d" % i, [128, 512]) for i in range(2)]
    P.dma("sp", lambda e: e.dma_start(out=ct[:], in_=cT[:, :, :]), writes=[ct])
    P.dma("sp", lambda e: e.dma_start(out=bt[:], in_=b[:, :]), writes=[bt])
    P.op("act", lambda e: e.activation(out=st[:], in_=ct[:], func=AF.Silu), reads=[ct], writes=[st])
    for c in range(24):
        wt = wb[c % 3]
        p_ = ps[c % 2]
        P.dma("sp", lambda e, wt=wt, c=c: e.dma_start(out=wt[:], in_=wv[:, :, c * 128:(c + 1) * 128]), writes=[wt])
        fns = [(lambda e, wt=wt, p_=p_, kc=kc: e.matmul(p_[:, 0:4], wt[:, kc, :], st[:, kc, :], start=(kc == 0), stop=(kc == 31)))
               for kc in range(32)]
        P.group("pe", fns, reads=[wt, st], writes=[p_])
        P.op("dve", lambda e, p_=p_, c=c: e.tensor_scalar(out=ot[:, c, :], in0=p_[:, 0:4], scalar1=bt[:, c:c + 1], scalar2=None, op0=ALU.add),
             reads=[p_, bt], writes=[ot])
    P.dma("sp", lambda e: e.dma_start(out=out[:, :, :], in_=ot[:]), reads=[ot])


def build_B(ntok=2048, TT=512):
    NTT = ntok // TT
    nc = bass.Bass("TRN2", target_bir_lowering=False)
    xT = nc.dram_tensor("xT", [D, ntok], F32, kind="ExternalInput").ap()
    oRT = nc.dram_tensor("oRT", [2048, ntok], BF16, kind="ExternalInput").ap()
    oAT = nc.dram_tensor("oAT", [2048, ntok], BF16, kind="ExternalInput").ap()
    modB = nc.dram_tensor("modB", [128, 6, 32], F32, kind="ExternalInput").ap()
    lnp = nc.dram_tensor("lnp", [128, 4, 32], F32, kind="ExternalInput").ap()
    wg = nc.dram_tensor("wg", [D, 2 * D], F32, kind="ExternalInput").ap()
    wro = nc.dram_tensor("wro", [2048, D], F32, kind="ExternalInput").ap()
    wao = nc.dram_tensor("wao", [2048, D], F32, kind="ExternalInput").ap()
    wout = nc.dram_tensor("wout", [D, D], F32, kind="ExternalInput").ap()
    wfg = nc.dram_tensor("wfg", [D, DFF], F32, kind="ExternalInput").ap()
    wfu = nc.dram_tensor("wfu", [D, DFF], F32, kind="ExternalInput").ap()
    wfd = nc.dram_tensor("wfd", [DFF, D], F32, kind="ExternalInput").ap()
    outT = nc.dram_tensor("outT", [D, ntok], F32, kind="ExternalOutput").ap()
    P = Prog(nc)
    P.run(lambda P: _B_body(P, NTT, TT, xT, oRT, oAT, modB, lnp, wg, wro, wao, wout, wfg, wfu, wfd, outT))
    return nc


def _B_body(P, NTT, TT, xT, oRT, oAT, modB, lnp, wg, wro, wao, wout, wfg, wfu, wfd, outT):
    KCv = lambda w_: w_.rearrange("(kc p) c -> p kc c", p=128)
    xTv, oRv, oAv = KCv(xT), KCv(oRT), KCv(oAT)
    wgv, wrov, waov, woutv, wfgv, wfuv, wfdv = map(KCv, (wg, wro, wao, wout, wfg, wfu, wfd))

    X = P.sb("X", [128, 32, TT], F32)
    H = P.sb("H", [128, 32, TT], BF16)
    A = P.sb("A", [128, 32, TT], BF16)
    OR = P.sb("OR", [128, 16, TT], BF16)
    OA = P.sb("OA", [128, 16, TT], BF16)
    WB = [P.sb("WB%d" % i, [128, 4096], BF16) for i in range(3)]
    modt = P.sb("modt", [128, 6, 32], F32)
    modp = P.sb("modp", [128, 2, 32], F32)
    lnt = P.sb("lnt", [128, 4, 32], F32)
    ones = P.sb("ones", [128, 128], F32)
    tmp = [P.sb("tmp%d" % i, [128, TT], F32) for i in range(4)]
    mean = P.sb("mean", [128, TT], F32)
    rstd = P.sb("rstd", [128, TT], F32)
    PS = [P.ps("ps%d" % i, [128, 512]) for i in range(8)]
    wcnt = [0]

    def wload(wv_, kc0, nkc, c0, ncols):
        t = WB[wcnt[0] % 3]
        wcnt[0] += 1
        P.dma("pool", lambda e: e.dma_start(out=t[:, 0:nkc * ncols].rearrange("p (k c) -> p k c", c=ncols),
                                            in_=wv_[:, kc0:kc0 + nkc, c0:c0 + ncols]), writes=[t])
        return t, t.h[:, 0:nkc * ncols].rearrange("p (k c) -> p k c", c=ncols)

    P.dma("sp", lambda e: e.dma_start(out=modt[:], in_=modB[:, :, :]), writes=[modt])
    P.dma("sp", lambda e: e.dma_start(out=lnt[:], in_=lnp[:, :, :]), writes=[lnt])
    P.op("dve", lambda e: e.memset(ones[:], 1.0), writes=[ones])
    P.op("dve", lambda e: e.tensor_scalar(out=modp[:, 0, :], in0=modt[:, 0, :], scalar1=1.0, scalar2=None, op0=ALU.add), reads=[modt], writes=[modp])
    P.op("dve", lambda e: e.tensor_scalar(out=modp[:, 1, :], in0=modt[:, 4, :], scalar1=1.0, scalar2=None, op0=ALU.add), reads=[modt], writes=[modp])

    def mm_group(ps_, wtile, wview, rhs_tl, rhs_fn, nkc, col0, ncols=128):
        fns = [(lambda e, kc=kc: e.matmul(ps_[:, 0:TT], wview[:, kc, col0:col0 + ncols], rhs_fn(kc), start=(kc == 0), stop=(kc == nkc - 1)))
               for kc in range(nkc)]
        P.group("pe", fns, reads=[wtile, rhs_tl], writes=[ps_])

    def layer_norm(gi, bi):
        s_ps, q_ps = PS[0], PS[1]
        for dt in range(DT):
            sq = tmp[dt % 2]
            P.op("act", lambda e, dt=dt, sq=sq: e.activation(out=sq[:], in_=X[:, dt, :], func=AF.Square), reads=[X], writes=[sq])
            P.group("pe", [lambda e, dt=dt: e.matmul(s_ps[:, 0:TT], ones[:], X[:, dt, :], start=(dt == 0), stop=(dt == DT - 1))],
                    reads=[ones, X], writes=[s_ps])
            P.group("pe", [lambda e, dt=dt, sq=sq: e.matmul(q_ps[:, 0:TT], ones[:], sq[:], start=(dt == 0), stop=(dt == DT - 1))],
                    reads=[ones, sq], writes=[q_ps])
        P.op("dve", lambda e: e.tensor_scalar(out=mean[:], in0=s_ps[:, 0:TT], scalar1=1.0 / D, scalar2=None, op0=ALU.mult), reads=[s_ps], writes=[mean])
        m2 = tmp[2]
        P.op("dve", lambda e: e.tensor_tensor(out=m2[:], in0=mean[:], in1=mean[:], op=ALU.mult), reads=[mean], writes=[m2])
        var = tmp[3]
        P.op("dve", lambda e: e.scalar_tensor_tensor(out=var[:], in0=q_ps[:, 0:TT], scalar=1.0 / D, in1=m2[:], op0=ALU.mult, op1=ALU.subtract),
             reads=[q_ps, m2], writes=[var])
        P.op("act", lambda e: e.activation(out=var[:], in_=var[:], func=AF.Sqrt, bias=eps_t[:, 0:1], scale=1.0), reads=[var, eps_t], writes=[var])
        P.op("dve", lambda e: e.reciprocal(out=rstd[:], in_=var[:]), reads=[var], writes=[rstd])
        for dt in range(DT):
            P.op("dve", lambda e, dt=dt: e.tensor_tensor(out=X[:, dt, :], in0=X[:, dt, :], in1=mean[:], op=ALU.subtract), reads=[X, mean], writes=[X])
            P.op("pool", lambda e, dt=dt: e.tensor_tensor(out=X[:, dt, :], in0=X[:, dt, :], in1=rstd[:], op=ALU.mult), reads=[X, rstd], writes=[X])
            P.op("act", lambda e, dt=dt: e.activation(out=X[:, dt, :], in_=X[:, dt, :], func=AF.Identity, bias=lnt[:, bi, dt:dt + 1], scale=lnt[:, gi, dt:dt + 1]),
                 reads=[X, lnt], writes=[X])

    eps_t = P.sb("eps_t", [128, 1], F32)
    P.op("dve", lambda e: e.memset(eps_t[:], LN_EPS), writes=[eps_t])

    for tt in range(NTT):
        t0 = tt * TT
        P.dma("sp", lambda e, t0=t0: e.dma_start(out=X[:], in_=xTv[:, :, t0:t0 + TT]), writes=[X])
        P.dma("sp", lambda e, t0=t0: e.dma_start(out=OR[:], in_=oRv[:, :, t0:t0 + TT]), writes=[OR])
        P.dma("sp", lambda e, t0=t0: e.dma_start(out=OA[:], in_=oAv[:, :, t0:t0 + TT]), writes=[OA])
        for dt in range(DT):
            P.op("act", lambda e, dt=dt: e.activation(out=H[:, dt, :], in_=X[:, dt, :], func=AF.Identity, bias=modt[:, 1, dt:dt + 1], scale=modp[:, 0, dt:dt + 1]),
                 reads=[X, modt, modp], writes=[H])
        for dt in range(DT):
            c0 = dt * 128
            w1, v1 = wload(wgv, 0, 32, c0, 128)
            mm_group(PS[0], w1, v1, H, lambda kc: H[:, kc, :], 32, 0)
            w2, v2 = wload(wgv, 0, 32, D + c0, 128)
            mm_group(PS[1], w2, v2, H, lambda kc: H[:, kc, :], 32, 0)
            w3, v3 = wload(wrov, 0, 16, c0, 128)
            mm_group(PS[2], w3, v3, OR, lambda kc: OR[:, kc, :], 16, 0)
            w4, v4 = wload(waov, 0, 16, c0, 128)
            mm_group(PS[3], w4, v4, OA, lambda kc: OA[:, kc, :], 16, 0)
            P.op("act", lambda e: e.activation(out=tmp[0][:], in_=PS[0][:, 0:TT], func=AF.Sigmoid), reads=[PS[0]], writes=[tmp[0]])
            P.op("act", lambda e: e.activation(out=tmp[1][:], in_=PS[1][:, 0:TT], func=AF.Sigmoid), reads=[PS[1]], writes=[tmp[1]])
            P.op("dve", lambda e: e.tensor_tensor(out=tmp[0][:], in0=tmp[0][:], in1=PS[2][:, 0:TT], op=ALU.mult), reads=[tmp[0], PS[2]], writes=[tmp[0]])
            P.op("dve", lambda e: e.tensor_tensor(out=tmp[1][:], in0=tmp[1][:], in1=PS[3][:, 0:TT], op=ALU.mult), reads=[tmp[1], PS[3]], writes=[tmp[1]])
            P.op("pool", lambda e, dt=dt: e.tensor_tensor(out=A[:, dt, :], in0=tmp[0][:], in1=tmp[1][:], op=ALU.add), reads=[tmp[0], tmp[1]], writes=[A])
        for dt in range(DT):
            w1, v1 = wload(woutv, 0, 32, dt * 128, 128)
            ps_ = PS[4 + dt % 2]
            mm_group(ps_, w1, v1, A, lambda kc: A[:, kc, :], 32, 0)
            P.op("dve", lambda e, dt=dt, ps_=ps_: e.tensor_scalar(out=tmp[2][:], in0=ps_[:, 0:TT], scalar1=modt[:, 2, dt:dt + 1], scalar2=None, op0=ALU.mult),
                 reads=[ps_, modt], writes=[tmp[2]])
            P.op("dve", lambda e, dt=dt: e.scalar_tensor_tensor(out=X[:, dt, :], in0=X[:, dt, :], scalar=ALPHA, in1=tmp[2][:], op0=ALU.mult, op1=ALU.add),
                 reads=[X, tmp[2]], writes=[X])
        layer_norm(0, 1)
        for dt in range(DT):
            P.op("act", lambda e, dt=dt: e.activation(out=H[:, dt, :], in_=X[:, dt, :], func=AF.Identity, bias=modt[:, 3, dt:dt + 1], scale=modp[:, 1, dt:dt + 1]),
                 reads=[X, modt, modp], writes=[H])
        for half, (fs, fn_) in enumerate([(0, 29), (29, 29), (58, 28)]):
            for fl in range(fn_):
                f0 = (fs + fl) * 128
                w1, v1 = wload(wfgv, 0, 32, f0, 128)
                pg = PS[(fl % 2) * 2]
                pu = PS[(fl % 2) * 2 + 1]
                mm_group(pg, w1, v1, H, lambda kc: H[:, kc, :], 32, 0)
                w2, v2 = wload(wfuv, 0, 32, f0, 128)
                mm_group(pu, w2, v2, H, lambda kc: H[:, kc, :], 32, 0)
                tm = tmp[fl % 2]
                P.op("act", lambda e, pg=pg, tm=tm: e.activation(out=tm[:], in_=pg[:, 0:TT], func=AF.Silu), reads=[pg], writes=[tm])
                P.op("dve", lambda e, pu=pu, tm=tm, fl=fl: e.tensor_tensor(out=A[:, fl, :], in0=tm[:], in1=pu[:, 0:TT], op=ALU.mult), reads=[tm, pu], writes=[A])
            for dt in range(DT):
                w1, v1 = wload(wfdv, fs, fn_, dt * 128, 128)
                ps_ = PS[4 + dt % 2]
                mm_group(ps_, w1, v1, A, lambda kc: A[:, kc, :], fn_, 0)
                P.op("dve", lambda e, dt=dt, ps_=ps_: e.tensor_scalar(out=tmp[2][:], in0=ps_[:, 0:TT], scalar1=modt[:, 5, dt:dt + 1], scalar2=None, op0=ALU.mult),
                     reads=[ps_, modt], writes=[tmp[2]])
                sc = ALPHA if half == 0 else 1.0
                P.op("dve", lambda e, dt=dt, sc=sc: e.scalar_tensor_tensor(out=X[:, dt, :], in0=X[:, dt, :], scalar=sc, in1=tmp[2][:], op0=ALU.mult, op1=ALU.add),
                     reads=[X, tmp[2]], writes=[X])
        layer_norm(2, 3)
        P.dma("sp", lambda e, t0=t0: e.dma_start(out=outT.rearrange("(kc p) c -> p kc c", p=128)[:, :, t0:t0 + TT], in_=X[:]), reads=[X])


import numpy as np

D = 4096
L = 256
NT_A = 29
GN_EPS = 64e-5
LOGW_SCALE = -float(np.exp(-0.5))
ATT_SCALE = float(128 ** -0.5)


def build_A(T):
    nc = bass.Bass("TRN2", target_bir_lowering=False)
    TS = L + T
    TW = (L + 2) + (T + 2)
    dt_in = lambda name, shape, dt=F32: nc.dram_tensor(name, list(shape), dt, kind="ExternalInput").ap()
    io = dict(
        xT=dt_in("xT", [D, T]), ctxT=dt_in("ctxT", [D, L]), modA=dt_in("modA", [128, 4, 32]),
        wA=dt_in("wA", [D, NT_A * 128]), shiftA=dt_in("shiftA", [128, 18, 3]),
        vecs=dt_in("vecs", [128, 11, 4]),
        wup=dt_in("wup", [2, 96, 512]), aup=dt_in("aup", [2, 96, 512]), gup=dt_in("gup", [256, 512]),
        cosT=dt_in("cosT", [128, T]), sinT=dt_in("sinT", [128, T]),
        sinkb=dt_in("sinkb", [128, 512]),
        consts=dt_in("consts", [128, 8, 128]),
        amask=dt_in("amask", [128, 2, 512]),
    )
    io["oRT"] = nc.dram_tensor("oRT", [512, T], BF16, kind="ExternalOutput").ap()
    io["oAT"] = nc.dram_tensor("oAT", [512, T], BF16, kind="ExternalOutput").ap()
    scr = dict(
        hT=nc.dram_tensor("s_hT", [D, TS], BF16, kind="Internal").ap(),
        rawP=nc.dram_tensor("s_rawP", [NT_A, 128, TW], F32, kind="Internal").ap(),
        FE=nc.dram_tensor("s_FE", [4, 2, 128, 6, TS], BF16, kind="Internal").ap(),
        FV=nc.dram_tensor("s_FV", [4, 128, TS], BF16, kind="Internal").ap(),
        EPI=nc.dram_tensor("s_EPI", [4, 128, 2, T], F32, kind="Internal").ap(),
        YB=nc.dram_tensor("s_YB", [4, T // 128, 128, 128], F32, kind="Internal").ap(),
    )
    P = Prog(nc)
    P.run(lambda P: _A_body(P, T, io, scr))
    return nc


def _A_body(P, T, io, scr):
    TS = L + T
    TW = (L + 2) + (T + 2)
    NB = T // 512
    NCH = TS // 128
    blocks = [(1, 0, L, True)] + [(L + 3 + 512 * b, L + 512 * b, 512, False) for b in range(NB)]
    rawP, FE, FV, EPI, YB, hT = scr["rawP"], scr["FE"], scr["FV"], scr["EPI"], scr["YB"], scr["hT"]
    KCv = lambda w_: w_.rearrange("(kc p) c -> p kc c", p=128)

    modt = P.sb("modt", [128, 4, 32], F32)
    modp = P.sb("modp", [128, 2, 32], F32)
    vec = P.sb("vec", [128, 11, 4], F32)
    shf = P.sb("shf", [128, 18, 3], F32)
    cst = P.sb("cst", [128, 8, 128], F32)
    cb = P.sb("cb", [128, 10, 128], BF16)
    EG = P.sb("EG", [128, 4, 2, NCH], F32)
    zer = P.sb("zer", [128, 4], F32)
    epsg = P.sb("epsg", [128, 1], F32)
    P.dma("sp", lambda e: e.dma_start(out=modt[:], in_=io["modA"][:, :, :]), writes=[modt])
    P.dma("sp", lambda e: e.dma_start(out=vec[:], in_=io["vecs"][:, :, :]), writes=[vec])
    P.dma("sp", lambda e: e.dma_start(out=shf[:], in_=io["shiftA"][:, :, :]), writes=[shf])
    P.dma("sp", lambda e: e.dma_start(out=cst[:], in_=io["consts"][:, :, :]), writes=[cst])
    P.op("dve", lambda e: e.memset(zer[:], 0.0), writes=[zer])
    P.op("dve", lambda e: e.memset(epsg[:], GN_EPS), writes=[epsg])
    P.ts("dve", modp[:, 0, :], modt[:, 0, :], 1.0, ALU.add, [modt], [modp])
    P.ts("dve", modp[:, 1, :], modt[:, 2, :], 1.0, ALU.add, [modt], [modp])
    P.ts("dve", vec[:, 9, :], vec[:, 5, :], -1.0, ALU.mult, [vec], [vec], s2=1.0, op1=ALU.add)
    P.copy("dve", cb[:, 0:5, :], cst[:, 0:5, :], [cst], [cb])
    P.ts("dve", cb[:, 5:9, :], cst[:, 0:4, :], -1.0, ALU.mult, [cst], [cb])
    for t_ in range(NT_A):
        for c in (0, L + 1, L + 2, L + 3 + T):
            P.dma("sp", lambda e, t_=t_, c=c: e.dma_start(out=rawP[t_, :, c:c + 1], in_=zer[:, 0:1], allow_slow_non_contiguous=True), reads=[zer])

    import os
    SKIP_EARLY = int(os.environ.get('SKIP_EARLY', '0'))
    with P.phase():
      if not SKIP_EARLY:
          X = P.sb("X", [128, 16, 512], F32)
          H = [P.sb("H%d" % i, [128, 32, 512], BF16) for i in range(2)]
          W = [P.sb("W%d" % i, [128, 32, 512], BF16) for i in range(2)]
          ST = [P.sb("ST%d" % i, [128, 512], F32) for i in range(4)]
          PS = [P.ps("psa%d" % i, [128, 512]) for i in range(8)]
          wv = KCv(io["wA"])
          xv, cv, hv = KCv(io["xT"]), KCv(io["ctxT"]), KCv(hT)
          groups = [(g * 4, min(4, NT_A - g * 4)) for g in range((NT_A + 3) // 4)]
          hcnt = 0
          pcnt = 0
          hT_tl = [Tl(None, "hT%d" % i) for i in range(len(blocks))]
          for gi, (t0, nt) in enumerate(groups):
              Wt = W[gi % 2]
              P.dma("pool", lambda e, Wt=Wt, t0=t0, nt=nt: e.dma_start(out=Wt[:, :, 0:nt * 128], in_=wv[:, :, t0 * 128:(t0 + nt) * 128]), writes=[Wt])
              for bi, (rc, so, bw, is_ctx) in enumerate(blocks):
                  Ht = H[hcnt % 2]
                  hcnt += 1
                  if gi == 0:
                      src = cv[:, :, 0:bw] if is_ctx else xv[:, :, so - L:so - L + bw]
                      mi = 1 if is_ctx else 0
                      for hf in range(2):
                          P.dma("sp", lambda e, src=src, bw=bw, hf=hf: e.dma_start(out=X[:, :, 0:bw], in_=src[:, hf * 16:(hf + 1) * 16, :]), writes=[X])
                          for k2 in range(16):
                              kc = hf * 16 + k2
                              P.act(Ht[:, kc, 0:bw], X[:, k2, 0:bw], AF.Identity, [X, modt, modp], [Ht],
                                    bias=modt[:, 2 * mi + 1, kc:kc + 1], scale=modp[:, mi, kc:kc + 1])
                      P.dma("sp", lambda e, Ht=Ht, so=so, bw=bw: e.dma_start(out=hv[:, :, so:so + bw], in_=Ht[:, :, 0:bw]), reads=[Ht], writes=[hT_tl[bi]])
                  else:
                      P.dma("sp", lambda e, Ht=Ht, so=so, bw=bw: e.dma_start(out=Ht[:, :, 0:bw], in_=hv[:, :, so:so + bw]), reads=[hT_tl[bi]], writes=[Ht])
                  for i in range(nt):
                      ps_ = PS[pcnt % 8]
                      st_ = ST[pcnt % 4]
                      pcnt += 1
                      P.mm(ps_, [(ps_[:, 0:bw], Wt[:, kc, i * 128:(i + 1) * 128], Ht[:, kc, 0:bw]) for kc in range(32)], [Wt, Ht])
                      if pcnt % 2 == 0:
                          P.copy("dve", st_[:, 0:bw], ps_[:, 0:bw], [ps_], [st_])
                      else:
                          P.copy("act", st_[:, 0:bw], ps_[:, 0:bw], [ps_], [st_])
                      P.dma("sp", lambda e, st_=st_, ti=t0 + i, rc=rc, bw=bw: e.dma_start(out=rawP[ti, :, rc:rc + bw], in_=st_[:, 0:bw]), reads=[st_])

    import os
    STOP = int(os.environ.get('STOP', '9'))
    if STOP < 1:
        return
    if SKIP_EARLY:
        with P.phase():
            _scan_phase(P, T, io, scr, cb, cst, EG, epsg)
        return
    with P.phase():
        ncnt = [0]

        def tmp(dt=F32):
            ncnt[0] += 1
            return P.sb("f%d" % ncnt[0], [128, 512], dt)
        RAW = [P.sb("RAW%d" % i, [128, 514], F32) for i in range(3)]
        rcnt = [0]
        wup = P.sb("wup_s", [96, 2, 512], BF16)
        aup = P.sb("aup_s", [96, 2, 512], BF16)
        gup = P.sb("gup_s", [128, 2, 512], BF16)
        P.dma("pool", lambda e: e.dma_start(out=wup[:], in_=io["wup"].rearrange("d r c -> r d c")), writes=[wup])
        P.dma("pool", lambda e: e.dma_start(out=aup[:], in_=io["aup"].rearrange("d r c -> r d c")), writes=[aup])
        P.dma("pool", lambda e: e.dma_start(out=gup[:], in_=io["gup"].rearrange("(k p) c -> p k c", p=128)), writes=[gup])
        bones = P.sb("bones", [128, 128], F32)
        P.copy("dve", bones[:], cst[:, 5, :], [cst], [bones])
        rmask = P.sb("rmask", [128, 512], F32)
        P.op("dve", lambda e: e.memset(rmask[:], 1.0), writes=[rmask])
        for c in range(4):
            P.op("dve", lambda e, c=c: e.memset(rmask[:, c * 128:c * 128 + 1], 0.0), writes=[rmask])
        PSf = [P.ps("psf%d" % i, [128, 512]) for i in range(6)]
        pfc = [0]

        def psf():
            pfc[0] += 1
            return PSf[pfc[0] % 6]
        twd = [tmp(BF16) for _ in range(2)]
        tad = [tmp(BF16) for _ in range(2)]
        tgd = [tmp(BF16) for _ in range(2)]
        cr, ck, cvv = tmp(), tmp(), tmp()
        kq, sq, rn, kk = tmp(), tmp(), tmp(), tmp()
        ic = [tmp(), tmp()]
        kd = [tmp(), tmp()]
        bb = [tmp(), tmp()]
        lw, pf, q_, s_, gi1 = tmp(), tmp(), tmp(), tmp(), tmp()
        E = [tmp() for _ in range(4)]
        gg, e1, e2, pr = tmp(), tmp(), tmp(), tmp()
        OUT = [tmp(BF16) for _ in range(7)]
        ocnt = [0]
        ecnt = [0]
        engs = ["dve", "pool"]

        def nx():
            ecnt[0] += 1
            return engs[ecnt[0] % 2]

        def conv(ti, dst, rc, bw, func=None, dst_reads=()):
            raw = RAW[rcnt[0] % 3]
            rcnt[0] += 1
            P.dma("sp", lambda e: e.dma_start(out=raw[:, 0:bw + 2], in_=rawP[ti, :, rc - 1:rc + bw + 1]), writes=[raw])
            t1 = cnvt
            P.ts("pool", t1[:, 0:bw], raw[:, 0:bw], shf[:, ti, 0:1], ALU.mult, [raw, shf], [t1])
            P.stt(t1[:, 0:bw], raw[:, 1:bw + 1], shf[:, ti, 1:2], t1[:, 0:bw], ALU.mult, ALU.add, [raw, shf, t1], [t1])
            if func is None:
                P.stt(dst[:, 0:bw], raw[:, 2:bw + 2], shf[:, ti, 2:3], t1[:, 0:bw], ALU.mult, ALU.add, [raw, shf, t1], [dst])
            else:
                P.stt(t1[:, 0:bw], raw[:, 2:bw + 2], shf[:, ti, 2:3], t1[:, 0:bw], ALU.mult, ALU.add, [raw, shf, t1], [t1])
                P.act(dst[:, 0:bw], t1[:, 0:bw], func, [t1], [dst])
        cnvt = tmp()

        for (rc, so, bw, is_ctx) in blocks:
            nch = bw // 128
            c0 = so // 128
            conv(12, twd[0], rc, bw, AF.Tanh)
            conv(13, twd[1], rc, bw, AF.Tanh)
            conv(14, tad[0], rc, bw, AF.Identity)
            conv(15, tad[1], rc, bw, AF.Identity)
            conv(16, tgd[0], rc, bw, AF.Sigmoid)
            conv(17, tgd[1], rc, bw, AF.Sigmoid)
            for hp in range(4):
                cs = slice(hp * 128, (hp + 1) * 128)
                conv(hp, cr, rc, bw)
                conv(4 + hp, ck, rc, bw)
                conv(8 + hp, cvv, rc, bw)
                P.ts(nx(), kq[:, 0:bw], ck[:, 0:bw], vec[:, 4, hp:hp + 1], ALU.mult, [ck, vec], [kq])
                P.act(sq[:, 0:bw], kq[:, 0:bw], AF.Square, [kq], [sq])
                p1 = psf()
                P.mm(p1, [(p1[:, 0:bw], bones[:], sq[:, 0:bw])], [bones, sq])
                P.ts("dve", rn[:, 0:bw], p1[:, 0:bw], 1e-24, ALU.max, [p1], [rn])
                P.act(rn[:, 0:bw], rn[:, 0:bw], AF.Sqrt, [rn], [rn])
                P.op("dve", lambda e: e.reciprocal(out=rn[:, 0:bw], in_=rn[:, 0:bw]), [rn], [rn])
                P.tt(nx(), kk[:, 0:bw], kq[:, 0:bw], rn[:, 0:bw], ALU.mult, [kq, rn], [kk])
                ov = OUT[ocnt[0] % 7]
                ocnt[0] += 1
                P.copy("act", ov[:, 0:bw], cvv[:, 0:bw], [cvv], [ov])
                P.dma("sp", lambda e, ov=ov, hp=hp: e.dma_start(out=FV[hp, :, so:so + bw], in_=ov[:, 0:bw]), reads=[ov])
                for d in range(2):
                    p2 = psf()
                    P.mm(p2, [(p2[:, 0:bw], aup[0:96, d, cs], tad[d][0:96, 0:bw])], [aup, tad[d]])
                    P.act(ic[d][:, 0:bw], p2[:, 0:bw], AF.Sigmoid, [p2, vec], [ic[d]], bias=vec[:, 2 + d, hp:hp + 1])
                    P.ts("dve", kd[d][:, 0:bw], ic[d][:, 0:bw], vec[:, 5, hp:hp + 1], ALU.mult, [ic[d], vec], [kd[d]], s2=vec[:, 9, hp:hp + 1], op1=ALU.add)
                    P.tt(nx(), kd[d][:, 0:bw], kd[d][:, 0:bw], ck[:, 0:bw], ALU.mult, [kd[d], ck], [kd[d]])
                    P.tt(nx(), bb[d][:, 0:bw], kk[:, 0:bw], ic[d][:, 0:bw], ALU.mult, [kk, ic[d]], [bb[d]])
                    p3 = psf()
                    P.mm(p3, [(p3[:, 0:bw], wup[0:96, d, cs], twd[d][0:96, 0:bw])], [wup, twd[d]])
                    P.act(lw[:, 0:bw], p3[:, 0:bw], AF.Sigmoid, [p3, vec], [lw], bias=vec[:, d, hp:hp + 1])
                    P.ts(nx(), lw[:, 0:bw], lw[:, 0:bw], LOGW_SCALE, ALU.mult, [lw], [lw])
                    P.op("dve", lambda e: e.tensor_tensor_scan(out=pf[:, 0:bw], data0=rmask[:, 0:bw], data1=lw[:, 0:bw], initial=0.0, op0=ALU.mult, op1=ALU.add),
                         [rmask, lw], [pf])
                    P.tt(nx(), q_[:, 0:bw], pf[:, 0:bw], lw[:, 0:bw], ALU.subtract, [pf, lw], [q_])
                    pf3 = pf.h[:, 0:bw].rearrange("p (c t) -> p c t", t=128)
                    gtot_b = pf3[:, :, 127:128].broadcast_to([128, nch, 128])
                    P.tt("dve", s_[:, 0:bw].rearrange("p (c t) -> p c t", t=128), gtot_b, pf3, ALU.subtract, [pf], [s_])
                    P.act(EG[:, hp, d, c0:c0 + nch], pf3[:, :, 127], AF.Exp, [pf], [EG])
                    if d == 0:
                        P.act(E[0][:, 0:bw], pf[:, 0:bw], AF.Exp, [pf], [E[0]], scale=-1.0)
                        P.act(E[1][:, 0:bw], q_[:, 0:bw], AF.Exp, [q_], [E[1]])
                        P.act(E[2][:, 0:bw], pf[:, 0:bw], AF.Exp, [pf], [E[2]])
                        P.act(E[3][:, 0:bw], s_[:, 0:bw], AF.Exp, [s_], [E[3]])
                    else:
                        P.tt(nx(), gi1[:, 0:bw], s_[:, 0:bw], lw[:, 0:bw], ALU.add, [s_, lw], [gi1])
                        P.act(E[0][:, 0:bw], gi1[:, 0:bw], AF.Exp, [gi1], [E[0]], scale=-1.0)
                        P.act(E[1][:, 0:bw], s_[:, 0:bw], AF.Exp, [s_], [E[1]])
                        P.act(E[2][:, 0:bw], gi1[:, 0:bw], AF.Exp, [gi1], [E[2]])
                        P.act(E[3][:, 0:bw], q_[:, 0:bw], AF.Exp, [q_], [E[3]])
                    prods = [(kd[d], E[0], 1.0), (bb[d], E[0], 1.0), (kk, E[1], 1.0), (cr, E[2], 1.0), (kd[d], E[3], 1.0), (bb[d], E[3], -1.0)]
                    for ai, (a_, e_, sg_) in enumerate(prods):
                        o_ = OUT[ocnt[0] % 7]
                        ocnt[0] += 1
                        if sg_ > 0:
                            P.tt(nx(), o_[:, 0:bw], a_[:, 0:bw], e_[:, 0:bw], ALU.mult, [a_, e_], [o_])
                        else:
                            P.stt(o_[:, 0:bw], a_[:, 0:bw], -1.0, e_[:, 0:bw], ALU.mult, ALU.mult, [a_, e_], [o_])
                        P.dma("sp", lambda e, o_=o_, hp=hp, d=d, ai=ai: e.dma_start(out=FE[hp, d, :, ai, so:so + bw], in_=o_[:, 0:bw]), reads=[o_])
                if not is_ctx:
                    p4 = psf()
                    P.mm(p4, [(p4[:, 0:bw], gup[:, k_, cs], tgd[k_][:, 0:bw]) for k_ in range(2)], [gup, tgd[0], tgd[1]])
                    P.copy("act", gg[:, 0:bw], p4[:, 0:bw], [p4], [gg])
                    P.tt(nx(), pr[:, 0:bw], kd[0][:, 0:bw], kd[1][:, 0:bw], ALU.add, [kd[0], kd[1]], [pr])
                    P.stt(pr[:, 0:bw], pr[:, 0:bw], vec[:, 8, hp:hp + 1], cr[:, 0:bw], ALU.mult, ALU.mult, [pr, vec, cr], [pr])
                    p5 = psf()
                    P.mm(p5, [(p5[:, 0:bw], bones[:], pr[:, 0:bw])], [bones, pr])
                    P.tt("dve", e2[:, 0:bw], cvv[:, 0:bw], p5[:, 0:bw], ALU.mult, [cvv, p5], [e2])
                    P.stt(e2[:, 0:bw], e2[:, 0:bw], vec[:, 7, hp:hp + 1], gg[:, 0:bw], ALU.add, ALU.mult, [e2, vec, gg], [e2])
                    P.ts(nx(), e1[:, 0:bw], gg[:, 0:bw], vec[:, 6, hp:hp + 1], ALU.mult, [gg, vec], [e1])
                    to = so - L
                    P.dma("sp", lambda e, hp=hp, to=to: e.dma_start(out=EPI[hp, :, 0, to:to + bw], in_=e1[:, 0:bw]), reads=[e1])
                    P.dma("sp", lambda e, hp=hp, to=to: e.dma_start(out=EPI[hp, :, 1, to:to + bw], in_=e2[:, 0:bw]), reads=[e2])

    if STOP < 2:
        return
    with P.phase():
        NQB = T // 128
        KT = P.sb("KT", [128, TS], BF16)
        VT = P.sb("VT", [128, NCH, 128], BF16)
        cosb = P.sb("cosb", [128, 512], F32)
        sinb = P.sb("sinb", [128, 512], F32)
        ra = [P.sb("ra%d" % i, [128, 512], F32) for i in range(2)]
        rb = [P.sb("rb%d" % i, [128, 512], F32) for i in range(2)]
        vb = P.sb("vb", [128, 512], BF16)
        idb = P.sb("idb", [128, 128], BF16)
        onesb = P.sb("onesb", [128, 128], BF16)
        am = P.sb("am", [128, 2, 512], F32)
        amb = P.sb("amb", [128, 2, 512], BF16)
        sk = P.sb("sk", [128, 512], F32)
        P.copy("dve", idb[:], cst[:, 4, :], [cst], [idb])
        P.op("dve", lambda e: e.memset(onesb[:], 1.0), writes=[onesb])
        P.dma("sp", lambda e: e.dma_start(out=am[:], in_=io["amask"][:, :, :]), writes=[am])
        P.copy("dve", amb[:], am[:], [am], [amb])
        P.dma("sp", lambda e: e.dma_start(out=sk[:], in_=io["sinkb"][:, :]), writes=[sk])
        P.act(sk[:], sk[:], AF.Exp, [sk], [sk])
        pst = [P.ps("pst%d" % i, [128, 128], F32) for i in range(2)]
        PSs = [P.ps("pss%d" % i, [128, 512]) for i in range(2)]
        PSd = [P.ps("psd%d" % i, [128, 512]) for i in range(2)]
        PSo = [P.ps("pso%d" % i, [128, 512]) for i in range(2)]
        tcn = 0
        for (rc, so, bw, is_ctx) in blocks:
            i_ = tcn % 2
            tcn += 1
            P.dma("sp", lambda e, i_=i_: e.dma_start(out=ra[i_][:, 0:bw], in_=rawP[26, :, rc:rc + bw]), writes=[ra[i_]])
            if is_ctx:
                P.copy("act", KT[:, so:so + bw], ra[i_][:, 0:bw], [ra[i_]], [KT])
            else:
                to = so - L
                P.dma("sp", lambda e, i_=i_: e.dma_start(out=rb[i_][:, 0:bw], in_=rawP[27, :, rc:rc + bw]), writes=[rb[i_]])
                P.dma("sp", lambda e: e.dma_start(out=cosb[:, 0:bw], in_=io["cosT"][:, to:to + bw]), writes=[cosb])
                P.dma("sp", lambda e: e.dma_start(out=sinb[:, 0:bw], in_=io["sinT"][:, to:to + bw]), writes=[sinb])
                P.tt("dve", ra[i_][:, 0:bw], ra[i_][:, 0:bw], cosb[:, 0:bw], ALU.mult, [ra[i_], cosb], [ra[i_]])
                P.tt("pool", rb[i_][:, 0:bw], rb[i_][:, 0:bw], sinb[:, 0:bw], ALU.mult, [rb[i_], sinb], [rb[i_]])
                P.tt("dve", KT[:, so:so + bw], ra[i_][:, 0:bw], rb[i_][:, 0:bw], ALU.add, [ra[i_], rb[i_]], [KT])
            P.dma("sp", lambda e, i_=i_: e.dma_start(out=rb[i_][:, 0:bw], in_=rawP[28, :, rc:rc + bw]), writes=[rb[i_]])
            P.copy("act", vb[:, 0:bw], rb[i_][:, 0:bw], [rb[i_]], [vb])
            for c in range(bw // 128):
                pt = pst[c % 2]
                P.mm(pt, [(pt[:], vb[:, c * 128:(c + 1) * 128], idb[:])], [vb, idb])
                P.copy("dve", VT[:, so // 128 + c, :], pt[:], [pt], [VT])
        QT = [P.sb("QT%d" % i, [128, 4, 128], BF16) for i in range(2)]
        qa = [P.sb("qa%d" % i, [128, 4, 128], F32) for i in range(2)]
        qp = [P.sb("qp%d" % i, [128, 4, 128], F32) for i in range(2)]
        cq = [P.sb("cq%d" % i, [128, 128], F32) for i in range(2)]
        sq_ = [P.sb("sq%d" % i, [128, 128], F32) for i in range(2)]
        PT_ = [P.sb("PT%d" % i, [128, 512], BF16) for i in range(10)]
        den = [P.sb("den%d" % i, [128, 512], F32) for i in range(2)]
        ob = [P.sb("ob%d" % i, [128, 512], BF16) for i in range(2)]
        oAv = io["oAT"].rearrange("(h d) t -> d h t", d=128)
        pc = 0
        for qb in range(NQB):
            i_ = qb % 2
            rq = L + 3 + qb * 128
            P.dma("sp", lambda e, i_=i_: e.dma_start(out=qa[i_][:], in_=rawP[18:22, :, rq:rq + 128].rearrange("h p t -> p h t")), writes=[qa[i_]])
            P.dma("sp", lambda e, i_=i_: e.dma_start(out=qp[i_][:], in_=rawP[22:26, :, rq:rq + 128].rearrange("h p t -> p h t")), writes=[qp[i_]])
            P.dma("sp", lambda e, i_=i_: e.dma_start(out=cq[i_][:], in_=io["cosT"][:, qb * 128:(qb + 1) * 128]), writes=[cq[i_]])
            P.dma("sp", lambda e, i_=i_: e.dma_start(out=sq_[i_][:], in_=io["sinT"][:, qb * 128:(qb + 1) * 128]), writes=[sq_[i_]])
            cb_ = cq[i_].h[:, :].unsqueeze(1).broadcast_to([128, 4, 128])
            sb_ = sq_[i_].h[:, :].unsqueeze(1).broadcast_to([128, 4, 128])
            P.tt("dve", qa[i_][:], qa[i_][:], cb_, ALU.mult, [qa[i_], cq[i_]], [qa[i_]])
            P.tt("pool", qp[i_][:], qp[i_][:], sb_, ALU.mult, [qp[i_], sq_[i_]], [qp[i_]])
            P.stt(QT[i_][:], qa[i_][:], 1.0, qp[i_][:], ALU.mult, ALU.add, [qa[i_], qp[i_]], [QT[i_]])
            kblocks = [(0, None), (1, None)]
            if qb > 0:
                kblocks.append((2 + qb - 1, 0))
            kblocks.append((2 + qb, None))
            if qb < NQB - 1:
                kblocks.append((2 + qb + 1, 1))
            pd, po = PSd[i_], PSo[i_]
            qflat = QT[i_].h[:, :, :].rearrange("p h t -> p (h t)")
            pts = []
            for ki, (kb, mk) in enumerate(kblocks):
                ps_ = PSs[pc % 2]
                pc += 1
                pt_ = PT_[i_ * 5 + ki]
                P.mm(ps_, [(ps_[:, :], KT[:, kb * 128:(kb + 1) * 128], qflat)], [KT, QT[i_]])
                P.act(pt_[:], ps_[:], AF.Exp, [ps_], [pt_], scale=ATT_SCALE)
                if mk is not None:
                    P.tt("pool", pt_[:], pt_[:], amb[:, mk, :], ALU.mult, [pt_, amb], [pt_])
                pts.append((pt_, kb))
            P.mm(pd, [(pd[:, :], onesb[:], pt_[:]) for (pt_, kb) in pts], [onesb] + [p_[0] for p_ in pts])
            P.mm(po, [(po[:, :], VT[:, kb, :], pt_[:]) for (pt_, kb) in pts], [VT] + [p_[0] for p_ in pts])
            P.tt("dve", den[i_][:], pd[:, :], sk[:], ALU.add, [pd, sk], [den[i_]])
            P.op("dve", lambda e, i_=i_: e.reciprocal(out=den[i_][:], in_=den[i_][:]), [den[i_]], [den[i_]])
            P.tt("dve", ob[i_][:], po[:, :], den[i_][:], ALU.mult, [po, den[i_]], [ob[i_]])
            P.dma("sp", lambda e, i_=i_, qb=qb: e.dma_start(out=oAv[:, :, qb * 128:(qb + 1) * 128], in_=ob[i_].h[:, :].rearrange("p (h t) -> p h t", t=128)), reads=[ob[i_]])

    if STOP < 3:
        return
    with P.phase():
        _scan_phase(P, T, io, scr, cb, cst, EG, epsg)


def _scan_phase(P, T, io, scr, cb, cst, EG, epsg):
    TS = L + T
    NCH = TS // 128
    FE, FV, EPI, YB = scr["FE"], scr["FV"], scr["EPI"], scr["YB"]
    oRT = io["oRT"]
    idf = P.sb("idf", [128, 128], F32)
    P.copy("dve", idf[:], cst[:, 4, :], [cst], [idf])
    bmask = P.sb("bmask", [128, 128], F32)
    P.copy("dve", bmask[:], cst[:, 5, :], [cst], [bmask])
    S32 = [P.sb("S32_%d" % hp, [128, 128], F32) for hp in range(4)]
    Sblk = [P.sb("Sblk%d" % hp, [128, 128], BF16) for hp in range(4)]
    PSP = [P.ps("ppb%d" % i, [128, 512]) for i in range(8)]
    pc = [0, 0]

    def pp():
        pc[0] += 1
        return PSP[pc[0] % 8]

    pb = pp

    class WS:
        pass
    WSs = []
    for hp in range(4):
        row = []
        for sl in range(2):
            w = WS()
            n = "%d_%d" % (hp, sl)
            w.FEt = P.sb("FEt" + n, [128, 6, 128], BF16)
            w.FVt = P.sb("FVt" + n, [128, 128], BF16)
            w.KGt = P.sb("KGt" + n, [128, 128], BF16)
            w.BGt = P.sb("BGt" + n, [128, 128], BF16)
            w.Vt = P.sb("Vt" + n, [128, 128], BF16)
            w.KKz = [P.sb("KKz%s_%d" % (n, h), [128, 128], BF16) for h in range(2)]
            w.Rz = [P.sb("Rz%s_%d" % (n, h), [128, 128], BF16) for h in range(2)]
            for h in range(2):
                P.op("pool", lambda e, w=w, h=h: e.memset(w.KKz[h][:], 0.0), writes=[w.KKz[h]])
                P.op("pool", lambda e, w=w, h=h: e.memset(w.Rz[h][:], 0.0), writes=[w.Rz[h]])
            w.AkkT = [P.sb("AkkT%s_%d" % (n, h), [128, 128], BF16) for h in range(2)]
            w.ArkT = [P.sb("ArkT%s_%d" % (n, h), [128, 128], BF16) for h in range(2)]
            w.Arbn = [P.sb("Arbn%s_%d" % (n, h), [128, 128], BF16) for h in range(2)]
            w.P0 = [P.sb("P0%s_%d" % (n, h), [128, 128], BF16) for h in range(2)]
            w.P0T = [P.sb("P0T%s_%d" % (n, h), [128, 128], BF16) for h in range(2)]
            w.PQ = [[P.sb("PQ%s_%d_%d" % (n, h, i), [128, 256], BF16) for i in range(2)] for h in range(2)]
            w.PT = [[P.sb("PT%s_%d_%d" % (n, h, i), [128, 128], BF16) for i in range(2)] for h in range(2)]
            w.TT = [P.sb("TT%s_%d" % (n, h), [128, 128], BF16) for h in range(2)]
            w.Xb = P.sb("Xb" + n, [128, 128], BF16)
            w.Ub = P.sb("Ub" + n, [128, 128], BF16)
            w.Ybt = P.sb("Ybt" + n, [128, 128], F32)
            w.epi = P.sb("epi" + n, [128, 2, 128], F32)
            w.ys = P.sb("ys" + n, [128, 128], F32)
            w.ysq = P.sb("ysq" + n, [128, 128], F32)
            w.st = P.sb("st" + n, [128, 8], F32)
            w.o32 = P.sb("o32" + n, [128, 128], F32)
            w.ob = P.sb("ob" + n, [128, 128], BF16)
            row.append(w)
        WSs.append(row)

    import os
    LIM_STEPS = int(os.environ.get('SCAN_STEPS', '100000'))
    LIM_SEQ = int(os.environ.get('SCAN_SEQ', '9'))
    LIM_PRE = int(os.environ.get('SCAN_PRE', '9'))
    for pas, d in enumerate([1, 0]):
        if pas >= int(os.environ.get('SCAN_PASSES', '2')):
            break
        if pas == 1:
            P.barrier()
        if d == 0:
            order = list(range(NCH))
            m_s, m_i, m_ns, m_ni, m_nsT = 0, 1, 5, 6, 7
        else:
            order = [1, 0] + list(range(NCH - 1, 1, -1))
            m_s, m_i, m_ns, m_ni, m_nsT = 2, 3, 7, 8, 5
        for hp in range(4):
            P.op("dve", lambda e, hp=hp: e.memset(S32[hp][:], 0.0), writes=[S32[hp]])
            P.op("pool", lambda e, hp=hp: e.memset(Sblk[hp][:], 0.0), writes=[Sblk[hp]])
        for step, c in enumerate(order):
            if step >= LIM_STEPS:
                break
            is_ctx = c < 2
            t0 = c * 128
            tl = t0 - L
            lc = c - 2
            for hp in range(4):
                w = WSs[hp][step % 2]
                P.dma("sp", lambda e: e.dma_start(out=w.FEt[:], in_=FE[hp, d, :, :, t0:t0 + 128]), writes=[w.FEt])
                P.dma("sp", lambda e: e.dma_start(out=w.FVt[:], in_=FV[hp, :, t0:t0 + 128]), writes=[w.FVt])
                if d == 0 and not is_ctx:
                    P.dma("sp", lambda e: e.dma_start(out=w.Ybt[:], in_=YB[hp, lc, :, :]), writes=[w.Ybt])
                    P.dma("sp", lambda e: e.dma_start(out=w.epi[:], in_=EPI[hp, :, :, tl:tl + 128]), writes=[w.epi])
                if LIM_PRE < 1:
                    continue
                for (src, srct, dst) in [(w.FEt[:, 4, :], w.FEt, w.KGt), (w.FEt[:, 5, :], w.FEt, w.BGt), (w.FVt[:, :], w.FVt, w.Vt)]:
                    b_ = pb()
                    P.mm(b_, [(b_[:, 0:128], src, cb[:, 4, :])], [srct, cb])
                    P.copy("act", dst[:], b_[:, 0:128], [b_], [dst])
                if LIM_PRE < 2:
                    continue
                for h in range(2):
                    ph = slice(64 * h, 64 * h + 64)
                    P.copy("pool", w.KKz[h][ph, :], w.FEt[ph, 2, :], [w.FEt], [w.KKz[h]])
                    if not is_ctx:
                        P.copy("pool", w.Rz[h][ph, :], w.FEt[ph, 3, :], [w.FEt], [w.Rz[h]])
                if LIM_PRE < 3:
                    continue
                for h in range(2):
                    Ka, Ba = w.FEt[:, 0, :], w.FEt[:, 1, :]
                    KKh, Rh = w.KKz[h], w.Rz[h]
                    jobs = [(Ka, KKh[:], w.AkkT[h], m_s), (Ba, KKh[:], w.P0[h], m_ns), (KKh[:], Ba, w.P0T[h], m_nsT)]
                    if not is_ctx:
                        jobs += [(Ka, Rh[:], w.ArkT[h], m_i), (Ba, Rh[:], w.Arbn[h], m_ni)]
                    for (l_, r_, dst, mi) in jobs:
                        p = pp()
                        P.mm(p, [(p[:, 0:128], l_, r_)], [w.FEt, KKh, Rh])
                        P.tt("dve", dst[:], p[:, 0:128], cb[:, mi, :], ALU.mult, [p, cb], [dst])
                    PQ, PT = w.PQ[h], w.PT[h]
                    P.tt("pool", PQ[0][:, 128:256], w.P0[h][:], cb[:, 4, :], ALU.add, [w.P0[h], cb], [PQ[0]])
                    p = pp()
                    P.mm(p, [(p[:, 0:128], w.P0T[h][:], w.P0[h][:])], [w.P0T[h], w.P0[h]])
                    P.copy("act", PQ[0][:, 0:128], p[:, 0:128], [p], [PQ[0]])
                    p = pp()
                    P.mm(p, [(p[:, 0:128], w.P0[h][:], w.P0T[h][:])], [w.P0T[h], w.P0[h]])
                    P.copy("act", PT[0][:], p[:, 0:128], [p], [PT[0]])
                    for j in range(1, 6):
                        cur, nxt = (j - 1) % 2, j % 2
                        p = pp()
                        P.mm(p, [(p[:, 0:256], PT[cur][:], PQ[cur][:, 0:256])], [PT[cur], PQ[cur]])
                        p2 = pp()
                        P.mm(p2, [(p2[:, 0:128], PQ[cur][:, 0:128], PT[cur][:])], [PT[cur], PQ[cur]])
                        P.copy("act", PQ[nxt][:, 0:128], p[:, 0:128], [p], [PQ[nxt]])
                        P.tt("dve", PQ[nxt][:, 128:256], p[:, 128:256], PQ[cur][:, 128:256], ALU.add, [p, PQ[cur]], [PQ[nxt]])
                        P.copy("act", PT[nxt][:], p2[:, 0:128], [p2], [PT[nxt]])
                    p = pp()
                    P.mm(p, [(p[:, 0:128], PT[1][:], PQ[1][:, 128:256])], [PT[1], PQ[1]])
                    P.tt("dve", w.TT[h][:], p[:, 0:128], PQ[1][:, 128:256], ALU.add, [p, PQ[1]], [w.TT[h]])
            for hp in range(4):
                if LIM_SEQ < 1:
                    break
                w = WSs[hp][step % 2]
                Sb, S3 = Sblk[hp], S32[hp]
                p = pp()
                P.mmx(p, [(p[:, 0:128], w.FEt[:, 2, :], Sb[:], True, False),
                          (p[:, 0:64], w.AkkT[0][:], w.Vt[:, 0:64], False, False),
                          (p[:, 64:128], w.AkkT[1][:], w.Vt[:, 64:128], False, True)], [w.FEt, Sb, w.AkkT[0], w.AkkT[1], w.Vt])
                P.copy("act", w.Xb[:], p[:, 0:128], [p], [w.Xb])
                p = pp()
                P.mmx(p, [(p[:, 0:64], w.TT[0][:], w.Xb[:, 0:64], True, True),
                          (p[:, 64:128], w.TT[1][:], w.Xb[:, 64:128], True, True)], [w.TT[0], w.TT[1], w.Xb])
                P.copy("act", w.Ub[:], p[:, 0:128], [p], [w.Ub])
                if not is_ctx:
                    py = pp()
                    P.mmx(py, [(py[:, 0:128], w.FEt[:, 3, :], Sb[:], True, False),
                               (py[:, 0:64], w.ArkT[0][:], w.Vt[:, 0:64], False, False),
                               (py[:, 0:64], w.Arbn[0][:], w.Ub[:, 0:64], False, False),
                               (py[:, 64:128], w.ArkT[1][:], w.Vt[:, 64:128], False, False),
                               (py[:, 64:128], w.Arbn[1][:], w.Ub[:, 64:128], False, True)],
                          [Sb, w.FEt, w.Vt, w.Ub, w.ArkT[0], w.ArkT[1], w.Arbn[0], w.Arbn[1]])
                p = pp()
                P.mmx(p, [(p[:, 0:128], w.KGt[:], w.Vt[:], True, False),
                          (p[:, 0:128], w.BGt[:], w.Ub[:], False, True)], [w.KGt, w.BGt, w.Vt, w.Ub])
                P.stt(S3[:], S3[:], EG[:, hp, d, c:c + 1], p[:, 0:128], ALU.mult, ALU.add, [S3, EG, p], [S3])
                P.tt("pool", S3[:], S3[:], bmask[:], ALU.mult, [S3, bmask], [S3])
                P.copy("pool", Sb[:], S3[:], [S3], [Sb])
                if is_ctx:
                    continue
                if d == 1:
                    P.copy("act", w.ys[:], py[:, 0:128], [py], [w.ys])
                    P.dma("sp", lambda e: e.dma_start(out=YB[hp, lc, :, :], in_=w.ys[:]), reads=[w.ys])
                else:
                    ys3 = w.ys.h[:, :].rearrange("p (h v) -> p h v", v=64)
                    sq3 = w.ysq.h[:, :].rearrange("p (h v) -> p h v", v=64)
                    st = w.st
                    P.tt("dve", w.ys[:], py[:, 0:128], w.Ybt[:], ALU.add, [py, w.Ybt], [w.ys])
                    P.op("dve", lambda e: e.tensor_reduce(out=st[:, 0:2], in_=ys3, axis=AX.X, op=ALU.add), [w.ys], [st])
                    P.act(w.ysq[:], w.ys[:], AF.Square, [w.ys], [w.ysq])
                    P.op("dve", lambda e: e.tensor_reduce(out=st[:, 2:4], in_=sq3, axis=AX.X, op=ALU.add), [w.ysq], [st])
                    P.ts("dve", st[:, 0:2], st[:, 0:2], 1.0 / 64, ALU.mult, [st], [st])
                    P.tt("dve", st[:, 4:6], st[:, 0:2], st[:, 0:2], ALU.mult, [st], [st])
                    P.stt(st[:, 2:4], st[:, 2:4], 1.0 / 64, st[:, 4:6], ALU.mult, ALU.subtract, [st], [st])
                    P.act(st[:, 2:4], st[:, 2:4], AF.Sqrt, [st, epsg], [st], bias=epsg[:, 0:1], scale=1.0)
                    P.op("dve", lambda e: e.reciprocal(out=st[:, 2:4], in_=st[:, 2:4]), [st], [st])
                    P.tt("dve", ys3, ys3, st[:, 0:2].unsqueeze(2).broadcast_to([128, 2, 64]), ALU.subtract, [w.ys, st], [w.ys])
                    P.tt("dve", ys3, ys3, st[:, 2:4].unsqueeze(2).broadcast_to([128, 2, 64]), ALU.mult, [w.ys, st], [w.ys])
                    pt = pp()
                    P.mm(pt, [(pt[:, 0:128], w.ys[:], idf[:])], [w.ys, idf])
                    P.tt("dve", w.o32[:], pt[:, 0:128], w.epi[:, 0, :], ALU.mult, [pt, w.epi], [w.o32])
                    P.tt("pool", w.ob[:], w.o32[:], w.epi[:, 1, :], ALU.add, [w.o32, w.epi], [w.ob])
                    P.dma("sp", lambda e: e.dma_start(out=oRT[hp * 128:(hp + 1) * 128, tl:tl + 128], in_=w.ob[:]), reads=[w.ob])


import numpy as np
import ml_dtypes

D = 4096
NR = 6784
OFF_Q, OFF_K, OFF_V, OFF_GR, OFF_GA = 6784, 8832, 9344, 9856, 13952


def pl(v):
    v = np.asarray(v, np.float32)
    return np.ascontiguousarray(v.reshape(v.shape[0], -1, 128).transpose(2, 0, 1))


def colsA(hg):
    cols = []
    ch = 512 * hg + np.arange(512)
    cols += list(ch)
    cols += list(2048 + ch)
    cols += list(4096 + ch)
    for base in (6144, 6240, 6336, 6432):
        c = base + np.arange(96)
        cols += list(c) + list(c[:32])
    cols += list(6528 + np.arange(256))
    rot = (np.arange(128) + 64) % 128
    for h in range(4):
        cols += list(OFF_Q + (4 * hg + h) * 128 + np.arange(128))
    for h in range(4):
        cols += list(OFF_Q + (4 * hg + h) * 128 + rot)
    cols += list(OFF_K + hg * 128 + np.arange(128))
    cols += list(OFF_K + hg * 128 + rot)
    cols += list(OFF_V + hg * 128 + np.arange(128))
    return np.array(cols, np.int64)


def rope_tables(T):
    rows = T // 64
    row = np.broadcast_to(np.arange(rows, dtype=np.float32)[:, None], (rows, 64)).reshape(T)
    col = np.broadcast_to(np.arange(64, dtype=np.float32)[None, :], (rows, 64)).reshape(T)
    inv = (np.float32(10000.0) ** (-np.arange(0, 64, 2, dtype=np.float32) / np.float32(64))).astype(np.float32)
    ang = np.concatenate([row[:, None] * inv, col[:, None] * inv], -1).astype(np.float32)
    cos, sin = np.cos(ang).astype(np.float32), np.sin(ang).astype(np.float32)
    cosT = np.ascontiguousarray(np.concatenate([cos, cos], 1).T)
    sinT = np.ascontiguousarray(np.concatenate([-sin, sin], 1).T)
    return cosT, sinT


def consts_A():
    c = np.zeros((128, 8, 128), np.float32)
    o = np.ones((128, 128), np.float32)
    c[:, 0] = np.triu(o, 1)
    c[:, 1] = np.triu(o, 0)
    c[:, 2] = np.tril(o, -1)
    c[:, 3] = np.tril(o, 0)
    c[:, 4] = np.eye(128, dtype=np.float32)
    c[:, 5] = np.kron(np.eye(2, dtype=np.float32), np.ones((64, 64), np.float32))
    am = np.zeros((128, 2, 512), np.float32)
    i = np.arange(128)[:, None]
    j = np.arange(128)[None, :]
    am[:, 0] = np.tile((i >= j).astype(np.float32), (1, 4))
    am[:, 1] = np.tile((i <= j).astype(np.float32), (1, 4))
    return c, am


def prep_A(inp, mod3, b, hg, T):
    sh1, sc1 = mod3[b, 0:D], mod3[b, D:2 * D]
    sh1c, sc1c = mod3[2, 0:D], mod3[2, D:2 * D]
    cols = colsA(hg)
    w_in = inp["w_in"][0]
    ch = 512 * hg + np.arange(512)
    sh = inp["rwkv_shift"][0][:, cols[:18 * 128]]
    shiftA = np.ascontiguousarray(sh.reshape(3, 18, 128).transpose(2, 1, 0))
    vl = [inp["rwkv_w0"][0][0], inp["rwkv_w0"][0][1], inp["rwkv_a0"][0][0], inp["rwkv_a0"][0][1], inp["rwkv_k_k"][0], inp["rwkv_k_a"][0],
          inp["rwkv_gn_g"][0], inp["rwkv_gn_b"][0], inp["rwkv_r_k"][0].reshape(-1)]
    vecs = np.zeros((128, 11, 4), np.float32)
    for s_, v in enumerate(vl):
        vecs[:, s_, :] = v[ch].reshape(4, 128).T
    cosT, sinT = rope_tables(T)
    c, am = consts_A()
    sink = inp["attn_sink"][0][4 * hg:4 * hg + 4]
    sinkb = np.ascontiguousarray(np.broadcast_to(np.repeat(sink, 128)[None, :], (128, 512))).astype(np.float32)
    return dict(
        xT=np.ascontiguousarray(inp["x"][b, :T, :].T), ctxT=np.ascontiguousarray(inp["ctx"][b].T),
        modA=pl(np.stack([sc1, sh1, sc1c, sh1c])), wA=np.ascontiguousarray(w_in[:, cols]), shiftA=shiftA, vecs=vecs,
        wup=np.ascontiguousarray(inp["rwkv_w_up"][0][:, :, ch]), aup=np.ascontiguousarray(inp["rwkv_a_up"][0][:, :, ch]),
        gup=np.ascontiguousarray(inp["rwkv_g_up"][0][:, ch]), cosT=cosT, sinT=sinT, sinkb=sinkb, consts=c, amask=am)


_CACHE = {}


def _get(name, fn):
    if name not in _CACHE:
        _CACHE[name] = fn()
    return _CACHE[name]


def kernel(**inp):
    inp = {k: np.asarray(v) for k, v in inp.items()}
    T = inp["x"].shape[1]
    NCORE = 8
    c3 = np.concatenate([inp["c"], inp["c_ctx"][None, :]], 0).astype(np.float32)
    cT = np.zeros((128, 32, 4), np.float32)
    cT[:, :, :3] = c3.T.reshape(32, 128, 3).transpose(1, 0, 2)
    w_ada = inp["w_ada"][0]
    b_ada = inp["b_ada"][0]
    maps = []
    for i in range(NCORE):
        maps.append({"cT": cT, "w": np.ascontiguousarray(w_ada[:, i * 3072:(i + 1) * 3072]),
                     "b": np.ascontiguousarray(b_ada[i * 3072:(i + 1) * 3072].reshape(24, 128).T)})
    nc0 = build_L0()
    res = run_bass_kernel_spmd(nc0, maps, core_ids=list(range(NCORE)))
    mod3 = np.concatenate([r["mod"].transpose(2, 1, 0).reshape(4, 3072)[:3] for r in res.results], 1)
    del maps
    ncA = build_A(T)
    maps = [prep_A(inp, mod3, i // 4, i % 4, T) for i in range(NCORE)]
    res = run_bass_kernel_spmd(ncA, maps, core_ids=list(range(NCORE)))
    ORT = [np.concatenate([res.results[b * 4 + hg]["oRT"] for hg in range(4)], 0) for b in range(2)]
    OAT = [np.concatenate([res.results[b * 4 + hg]["oAT"] for hg in range(4)], 0) for b in range(2)]
    del maps
    NT = T // 4
    ncB = build_B(ntok=NT)
    wg = np.ascontiguousarray(inp["w_in"][0][:, OFF_GR:OFF_GR + 2 * D])
    lnp = pl(np.stack([inp["ln1_g"][0], inp["ln1_b"][0], inp["ln2_g"][0], inp["ln2_b"][0]]))
    maps = []
    for i in range(NCORE):
        b, tq = i // 4, (i % 4) * NT
        m = mod3[b]
        modB = pl(np.stack([m[D:2 * D], m[0:D], m[2 * D:3 * D], m[3 * D:4 * D], m[4 * D:5 * D], m[5 * D:6 * D]]))
        maps.append({"xT": np.ascontiguousarray(inp["x"][b, tq:tq + NT, :].T),
                     "oRT": np.ascontiguousarray(ORT[b][:, tq:tq + NT]), "oAT": np.ascontiguousarray(OAT[b][:, tq:tq + NT]),
                     "modB": modB, "lnp": lnp, "wg": wg, "wro": inp["w_rwkv_o"][0], "wao": inp["w_att_o"][0], "wout": inp["w_out"][0],
                     "wfg": inp["w_ff_gate"][0], "wfu": inp["w_ff_up"][0], "wfd": inp["w_ff_down"][0]})
    res = run_bass_kernel_spmd(ncB, maps, core_ids=list(range(NCORE)))
    out = np.empty((2, T, D), np.float32)
    for i in range(NCORE):
        b, tq = i // 4, (i % 4) * NT
        out[b, tq:tq + NT, :] = res.results[i]["outT"].T
    return out
```
